# Optimizing a Trainium2 kernel written in Bass

```python
import math
import jax, jax.numpy as jnp
from jax import lax
import numpy as np

D_MODEL = 1024
BATCH = 2
SEQ = 8192
DEPTH = 4

N_MIXERS = 3
NORM_EPS = 1e-6
D_FF = ((-(-8 * D_MODEL // 3) + 255) // 256) * 256

D_RNN = 5 * D_MODEL // 4
LRU_HEADS = 10
LRU_HEAD_DIM = D_RNN // LRU_HEADS
CONV_WIDTH = 4
LRU_C = 8.0

NSA_HEADS = 16
NSA_KV_GROUPS = 4
NSA_HEAD_DIM = 64
CMP_LEN = 32
CMP_STRIDE = 16
SEL_LEN = 64
SEL_TOPN = 16
WINDOW = 512
Q_BLOCK = 128
FORCE_SCORE = 1e4
NSA_PROJ = NSA_HEADS * NSA_HEAD_DIM + 6 * NSA_KV_GROUPS * NSA_HEAD_DIM + 3 * NSA_HEADS

REL_BUCKETS = 32
REL_MAX_DIST = 128

GLA_HEADS = 4
GLA_DK = D_MODEL // 2
GLA_DV = D_MODEL
GLA_GATE_RANK = 16
GLA_TAU = 16.0
GLA_CHUNK = 64
GLA_PROJ = 2 * GLA_DK + 2 * GLA_DV + GLA_GATE_RANK

N_A = (DEPTH + 2) // 3
N_B = (DEPTH + 1) // 3
N_C = DEPTH // 3

kernel_name = 'hybrid_rglru_nsa_gla_interleaved'


def rmsnorm(x, g):
    xf = x.astype(jnp.float32)
    y = xf * lax.rsqrt(jnp.mean(xf * xf, axis=-1, keepdims=True) + NORM_EPS) * g.astype(jnp.float32)
    return y.astype(x.dtype)


def swiglu(h, w_gate, w_up, w_down):
    return (jax.nn.silu(h @ w_gate) * (h @ w_up)) @ w_down


def masked_softmax(logits, mask):
    lg = jnp.where(mask, logits.astype(jnp.float32), -jnp.inf)
    m = jnp.max(lg, axis=-1, keepdims=True)
    m = jnp.where(jnp.isfinite(m), m, 0.0)
    e = jnp.exp(lg - m)
    s = jnp.sum(e, axis=-1, keepdims=True)
    return e / jnp.where(s > 0, s, 1.0)


def rel_bucket(dist):
    n = jnp.maximum(dist, 0)
    max_exact = REL_BUCKETS // 2
    nf = jnp.maximum(n, 1).astype(jnp.float32)
    large = max_exact + (jnp.log(nf / max_exact) / math.log(REL_MAX_DIST / max_exact)
                         * (REL_BUCKETS - max_exact)).astype(jnp.int32)
    large = jnp.minimum(large, REL_BUCKETS - 1)
    return jnp.where(n < max_exact, n, large)


def causal_depthwise_conv(x, w, b):
    S = x.shape[1]
    xp = jnp.pad(x, ((0, 0), (CONV_WIDTH - 1, 0), (0, 0)))
    y = b
    for k in range(CONV_WIDTH):
        y = y + xp[:, k:k + S] * w[k]
    return y


def _linear_combine(left, right):
    a_l, b_l = left
    a_r, b_r = right
    return a_l * a_r, a_r * b_l + b_r


def rglru_mixer(h, w_in, conv_w, conv_b, w_a, b_a, w_x, b_x, lam, w_out):
    B, S, _ = h.shape
    gate_in, x_in = jnp.split(h @ w_in, 2, axis=-1)
    xc = causal_depthwise_conv(x_in, conv_w, conv_b)
    xh = xc.reshape(B, S, LRU_HEADS, LRU_HEAD_DIM)
    r = jax.nn.sigmoid((jnp.einsum('bshi,hij->bshj', xh, w_a).reshape(B, S, D_RNN) + b_a).astype(jnp.float32))
    i = jax.nn.sigmoid((jnp.einsum('bshi,hij->bshj', xh, w_x).reshape(B, S, D_RNN) + b_x).astype(jnp.float32))
    log_a = -LRU_C * r * jax.nn.softplus(-lam.astype(jnp.float32))
    a = jnp.exp(log_a)
    u = jnp.sqrt(-jnp.expm1(2.0 * log_a)) * (i * xc.astype(jnp.float32))
    _, hs = lax.associative_scan(_linear_combine, (a, u), axis=1)
    y = jax.nn.gelu(gate_in, approximate=True) * hs.astype(h.dtype)
    return y @ w_out


def nsa_compress(kv, pos, w1, b1, w2, b2):
    B, S, G, dh = kv.shape
    nb = (S - CMP_LEN) // CMP_STRIDE + 1
    idx = jnp.arange(nb)[:, None] * CMP_STRIDE + jnp.arange(CMP_LEN)[None, :]
    blocks = kv[:, idx] + pos[None, None, :, None, :]
    flat = jnp.moveaxis(blocks, 3, 2).reshape(B, nb, G, CMP_LEN * dh)
    return jax.nn.gelu(flat @ w1 + b1) @ w2 + b2


def cmp_to_sel_map(nb_cmp, nb_sel):
    start = jnp.arange(nb_cmp)[:, None] * CMP_STRIDE
    sel_start = jnp.arange(nb_sel)[None, :] * SEL_LEN
    return ((start < sel_start + SEL_LEN) & (start + CMP_LEN > sel_start)).astype(jnp.float32)


def nsa_mixer(h, w_in, cmp_pos, cmp_w1, cmp_b1, cmp_w2, cmp_b2, w_out, rel_bias):
    B, S, _ = h.shape
    G, HG, DH = NSA_KV_GROUPS, NSA_HEADS // NSA_KV_GROUPS, NSA_HEAD_DIM
    qw, kvw = NSA_HEADS * DH, NSA_KV_GROUPS * DH
    proj = h @ w_in
    q = proj[..., :qw].reshape(B, S, G, HG, DH) * DH ** -0.5
    kv = proj[..., qw:qw + 6 * kvw].reshape(B, S, 6, G, DH)
    k_c, v_c, k_s, v_s, k_w, v_w = [kv[:, :, j] for j in range(6)]
    gates = jax.nn.sigmoid(proj[..., qw + 6 * kvw:].reshape(B, S, G, HG, 3))

    kc = nsa_compress(k_c, cmp_pos[0], cmp_w1[0], cmp_b1[0], cmp_w2[0], cmp_b2[0])
    vc = nsa_compress(v_c, cmp_pos[1], cmp_w1[1], cmp_b1[1], cmp_w2[1], cmp_b2[1])
    nb_cmp = kc.shape[1]
    nb_sel = S // SEL_LEN
    n_sel = min(SEL_TOPN, nb_sel)
    sel_map = cmp_to_sel_map(nb_cmp, nb_sel)
    cmp_end = jnp.arange(nb_cmp) * CMP_STRIDE + CMP_LEN - 1

    def to_blocks(t):
        return jnp.moveaxis(t.reshape(B, nb_sel, SEL_LEN, G, DH), 3, 1).reshape(B, G, nb_sel, SEL_LEN * DH)

    ks_blk, vs_blk = to_blocks(k_s), to_blocks(v_s)
    kw_pad = jnp.pad(k_w, ((0, 0), (WINDOW, 0), (0, 0), (0, 0)))
    vw_pad = jnp.pad(v_w, ((0, 0), (WINDOW, 0), (0, 0), (0, 0)))
    table_g = jnp.transpose(rel_bias.reshape(REL_BUCKETS, G, HG), (1, 0, 2))
    blk_ids = jnp.arange(nb_sel)
    g_ids = jnp.arange(G)[None, :, None, None]

    def head_bias(dist):
        return jnp.transpose(rel_bias[rel_bucket(dist)].reshape(*dist.shape, G, HG), (2, 3, 0, 1))

    def query_block(qb):
        t0 = qb * Q_BLOCK
        tpos = t0 + jnp.arange(Q_BLOCK)
        qi = lax.dynamic_slice_in_dim(q, t0, Q_BLOCK, axis=1)
        gi = lax.dynamic_slice_in_dim(gates, t0, Q_BLOCK, axis=1)
        dist_c = tpos[:, None] - cmp_end[None, :]
        lg_c = jnp.einsum('bqghd,bngd->bghqn', qi, kc) + head_bias(dist_c)
        p_c = masked_softmax(lg_c, dist_c >= 0)
        o_c = jnp.einsum('bghqn,bngd->bqghd', p_c.astype(vc.dtype), vc)
        imp = jnp.einsum('bghqn,nm->bgqm', p_c, sel_map)
        cur = tpos // SEL_LEN
        valid = blk_ids[None, :] * SEL_LEN <= tpos[:, None]
        forced = (blk_ids[None, :] == 0) | (blk_ids[None, :] == cur[:, None]) | (blk_ids[None, :] == cur[:, None] - 1)
        score = jnp.where(valid, imp + jnp.where(forced, FORCE_SCORE, 0.0), -jnp.inf)
        _, idx = lax.top_k(score, n_sel)
        flat_idx = idx.reshape(B, G, Q_BLOCK * n_sel, 1)
        k_sel = jnp.take_along_axis(ks_blk, flat_idx, axis=2).reshape(B, G, Q_BLOCK, n_sel * SEL_LEN, DH)
        v_sel = jnp.take_along_axis(vs_blk, flat_idx, axis=2).reshape(B, G, Q_BLOCK, n_sel * SEL_LEN, DH)
        key_pos = (idx[..., None] * SEL_LEN + jnp.arange(SEL_LEN)).reshape(B, G, Q_BLOCK, n_sel * SEL_LEN)
        dist_s = tpos[None, None, :, None] - key_pos
        bias_s = jnp.moveaxis(table_g[g_ids, rel_bucket(dist_s)], -1, 3)
        lg_s = jnp.einsum('bqghd,bgqkd->bgqhk', qi, k_sel) + bias_s
        p_s = masked_softmax(lg_s, (dist_s >= 0)[:, :, :, None, :])
        o_s = jnp.einsum('bgqhk,bgqkd->bqghd', p_s.astype(v_sel.dtype), v_sel)
        kw = lax.dynamic_slice_in_dim(kw_pad, t0, Q_BLOCK + WINDOW, axis=1)
        vw = lax.dynamic_slice_in_dim(vw_pad, t0, Q_BLOCK + WINDOW, axis=1)
        key_pos_w = t0 - WINDOW + jnp.arange(Q_BLOCK + WINDOW)
        dist_w = tpos[:, None] - key_pos_w[None, :]
        mask_w = (dist_w >= 0) & (dist_w < WINDOW) & (key_pos_w[None, :] >= 0)
        lg_w = jnp.einsum('bqghd,bkgd->bghqk', qi, kw) + head_bias(dist_w)
        p_w = masked_softmax(lg_w, mask_w)
        o_w = jnp.einsum('bghqk,bkgd->bqghd', p_w.astype(vw.dtype), vw)
        return gi[..., 0:1] * o_c + gi[..., 1:2] * o_s + gi[..., 2:3] * o_w

    o = lax.map(query_block, jnp.arange(S // Q_BLOCK))
    o = jnp.moveaxis(o, 0, 1).reshape(B, S, NSA_HEADS * DH)
    return o @ w_out


def gla_mixer(h, w_in, w_g2, b_g2, norm_g, w_out):
    B, S, _ = h.shape
    H, C = GLA_HEADS, GLA_CHUNK
    dk, dv = GLA_DK // H, GLA_DV // H
    N = S // C
    proj = h @ w_in
    q, k, v, r, g_low = jnp.split(proj, [GLA_DK, 2 * GLA_DK, 2 * GLA_DK + GLA_DV, 2 * GLA_DK + 2 * GLA_DV], axis=-1)
    log_alpha = jax.nn.log_sigmoid((g_low @ w_g2 + b_g2).astype(jnp.float32)) / GLA_TAU

    def chunks(t, d):
        return jnp.transpose(t.astype(jnp.float32).reshape(B, N, C, H, d), (0, 3, 1, 2, 4))

    q = chunks(q, dk) * dk ** -0.5
    k, v, la = chunks(k, dk), chunks(v, dv), chunks(log_alpha, dk)
    b = jnp.cumsum(la, axis=3)
    b_last = b[:, :, :, -1:, :]
    q_dec = q * jnp.exp(b)
    k_inv = k * jnp.exp(-b)
    k_end = k * jnp.exp(b_last - b)
    causal = jnp.tril(jnp.ones((C, C), dtype=bool))
    att = jnp.where(causal, jnp.einsum('bhncd,bhnjd->bhncj', q_dec, k_inv), 0.0)
    o_intra = jnp.einsum('bhncj,bhnje->bhnce', att, v)

    def step(state, inp):
        q_c, k_c, v_c, dec = inp
        o = jnp.einsum('bhcd,bhde->bhce', q_c, state)
        state = state * dec[..., None] + jnp.einsum('bhcd,bhce->bhde', k_c, v_c)
        return state, o

    xs = (jnp.moveaxis(q_dec, 2, 0), jnp.moveaxis(k_end, 2, 0), jnp.moveaxis(v, 2, 0),
          jnp.moveaxis(jnp.exp(b_last[:, :, :, 0, :]), 2, 0))
    _, o_inter = lax.scan(step, jnp.zeros((B, H, dk, dv), jnp.float32), xs)
    o = o_intra + jnp.moveaxis(o_inter, 0, 2)
    o = jnp.transpose(o, (0, 2, 3, 1, 4)).reshape(B, S, H, dv)
    o = o * lax.rsqrt(jnp.mean(o * o, axis=-1, keepdims=True) + NORM_EPS) * norm_g.astype(jnp.float32)
    o = o.reshape(B, S, GLA_DV).astype(h.dtype) * jax.nn.silu(r)
    return o @ w_out


def setup_inputs(seed: int = 0) -> dict:
    key = jax.random.key(seed)
    keys = iter(jax.random.split(key, 40))
    f32 = jnp.float32

    def nrm(shape, fan_in, scale=1.0):
        return jax.random.normal(next(keys), shape, f32) * (scale * fan_in ** -0.5)

    def gain(shape):
        return 1.0 + 0.02 * jax.random.normal(next(keys), shape, f32)

    def small(shape, s=0.01):
        return s * jax.random.normal(next(keys), shape, f32)

    D = D_MODEL
    x = jax.random.normal(next(keys), (BATCH, SEQ, D), f32)
    rel_bias = small((REL_BUCKETS, NSA_HEADS), 0.2)
    final_norm = gain((D,))
    mix_norm = gain((DEPTH, D))
    ffn_norm = gain((DEPTH, D))
    ffn_w_gate = nrm((DEPTH, D, D_FF), D)
    ffn_w_up = nrm((DEPTH, D, D_FF), D)
    ffn_w_down = nrm((DEPTH, D_FF, D), D_FF)
    lru_w_in = nrm((N_A, D, 2 * D_RNN), D)
    lru_conv_w = nrm((N_A, CONV_WIDTH, D_RNN), CONV_WIDTH)
    lru_conv_b = small((N_A, D_RNN))
    lru_w_a = nrm((N_A, LRU_HEADS, LRU_HEAD_DIM, LRU_HEAD_DIM), LRU_HEAD_DIM)
    lru_b_a = small((N_A, D_RNN))
    lru_w_x = nrm((N_A, LRU_HEADS, LRU_HEAD_DIM, LRU_HEAD_DIM), LRU_HEAD_DIM)
    lru_b_x = small((N_A, D_RNN))
    u = jax.random.uniform(next(keys), (N_A, D_RNN), f32, 0.9, 0.999)
    p = u ** (1.0 / LRU_C)
    lru_lam = jnp.log(p) - jnp.log1p(-p)
    lru_w_out = nrm((N_A, D_RNN, D), D_RNN)
    nsa_w_in = nrm((N_B, D, NSA_PROJ), D)
    nsa_cmp_pos = small((N_B, 2, CMP_LEN, NSA_HEAD_DIM), 0.1)
    nsa_cmp_w1 = nrm((N_B, 2, CMP_LEN * NSA_HEAD_DIM, NSA_HEAD_DIM), CMP_LEN * NSA_HEAD_DIM)
    nsa_cmp_b1 = small((N_B, 2, NSA_HEAD_DIM))
    nsa_cmp_w2 = nrm((N_B, 2, NSA_HEAD_DIM, NSA_HEAD_DIM), NSA_HEAD_DIM)
    nsa_cmp_b2 = small((N_B, 2, NSA_HEAD_DIM))
    nsa_w_out = nrm((N_B, NSA_HEADS * NSA_HEAD_DIM, D), NSA_HEADS * NSA_HEAD_DIM)
    gla_w_in = nrm((N_C, D, GLA_PROJ), D)
    gla_w_g2 = nrm((N_C, GLA_GATE_RANK, GLA_DK), GLA_GATE_RANK)
    gla_b_g2 = small((N_C, GLA_DK), 0.1)
    gla_norm = gain((N_C, GLA_DV // GLA_HEADS))
    gla_w_out = nrm((N_C, GLA_DV, D), GLA_DV)
    return {
        'x': x, 'rel_bias': rel_bias, 'final_norm': final_norm, 'mix_norm': mix_norm, 'ffn_norm': ffn_norm,
        'ffn_w_gate': ffn_w_gate, 'ffn_w_up': ffn_w_up, 'ffn_w_down': ffn_w_down,
        'lru_w_in': lru_w_in, 'lru_conv_w': lru_conv_w, 'lru_conv_b': lru_conv_b, 'lru_w_a': lru_w_a,
        'lru_b_a': lru_b_a, 'lru_w_x': lru_w_x, 'lru_b_x': lru_b_x, 'lru_lam': lru_lam, 'lru_w_out': lru_w_out,
        'nsa_w_in': nsa_w_in, 'nsa_cmp_pos': nsa_cmp_pos, 'nsa_cmp_w1': nsa_cmp_w1, 'nsa_cmp_b1': nsa_cmp_b1,
        'nsa_cmp_w2': nsa_cmp_w2, 'nsa_cmp_b2': nsa_cmp_b2, 'nsa_w_out': nsa_w_out,
        'gla_w_in': gla_w_in, 'gla_w_g2': gla_w_g2, 'gla_b_g2': gla_b_g2, 'gla_norm': gla_norm, 'gla_w_out': gla_w_out,
    }


def reference(x, rel_bias, final_norm, mix_norm, ffn_norm, ffn_w_gate, ffn_w_up, ffn_w_down,
              lru_w_in, lru_conv_w, lru_conv_b, lru_w_a, lru_b_a, lru_w_x, lru_b_x, lru_lam, lru_w_out,
              nsa_w_in, nsa_cmp_pos, nsa_cmp_w1, nsa_cmp_b1, nsa_cmp_w2, nsa_cmp_b2, nsa_w_out,
              gla_w_in, gla_w_g2, gla_b_g2, gla_norm, gla_w_out):
    h = x
    for layer in range(DEPTH):
        kind = layer % N_MIXERS
        j = layer // N_MIXERS
        hn = rmsnorm(h, mix_norm[layer])
        if kind == 0:
            y = rglru_mixer(hn, lru_w_in[j], lru_conv_w[j], lru_conv_b[j], lru_w_a[j], lru_b_a[j],
                            lru_w_x[j], lru_b_x[j], lru_lam[j], lru_w_out[j])
        elif kind == 1:
            y = nsa_mixer(hn, nsa_w_in[j], nsa_cmp_pos[j], nsa_cmp_w1[j], nsa_cmp_b1[j], nsa_cmp_w2[j],
                          nsa_cmp_b2[j], nsa_w_out[j], rel_bias)
        else:
            y = gla_mixer(hn, gla_w_in[j], gla_w_g2[j], gla_b_g2[j], gla_norm[j], gla_w_out[j])
        h = h + y
        h = h + swiglu(rmsnorm(h, ffn_norm[layer]), ffn_w_gate[layer], ffn_w_up[layer], ffn_w_down[layer])
    return rmsnorm(h, final_norm)
```

```python
import numpy as np
from contextlib import ExitStack
import concourse.bass as bass
import concourse.mybir as mybir
from concourse.bass_utils import run_bass_kernel_spmd

F32, BF16 = mybir.dt.float32, mybir.dt.bfloat16
AF = mybir.ActivationFunctionType
ALU = mybir.AluOpType

D = 1024
DFF = 2816
S = 8192
NB = 2
NCORE = 8
TPC = 2048
EPS = 1e-6


class Dep:
    __slots__ = ("w", "r")

    def __init__(self):
        self.w = None
        self.r = {}


NDMA = 24


class Sched:
    def __init__(self, nc, es):
        self.nc = nc
        self.eng = dict(pe=nc.tensor, act=nc.scalar, dve=nc.vector, pool=nc.gpsimd, sp=nc.sync)
        self.sems = {k: es.enter_context(nc.semaphore("s_" + k)) for k in ("pe", "act", "dve", "pool")}
        self.cnt = {k: 0 for k in self.sems}
        self.seen = {k: {} for k in self.eng}
        self.dma_sems = [es.enter_context(nc.semaphore("d%d" % i)) for i in range(NDMA)]
        self.dma_cnt = [0] * NDMA
        self.dma_rr = 0
        self.out_events = []

    def _sem(self, k):
        return self.dma_sems[k[1]] if isinstance(k, tuple) else self.sems[k]

    def _wait(self, e, ev):
        if ev is None:
            return
        k, v = ev
        if k == e and e == "pe":
            return
        if self.seen[e].get(k, 0) >= v:
            return
        self.seen[e][k] = v
        self.eng[e].wait_ge(self._sem(k), v)

    def _pre(self, e, reads, writes):
        for d in reads:
            self._wait(e, d.w)
        for d in writes:
            self._wait(e, d.w)
            for k, v in d.r.items():
                self._wait(e, (k, v))

    def _post(self, ev, reads, writes):
        k, v = ev
        for d in reads:
            if d.r.get(k, 0) < v:
                d.r[k] = v
        for d in writes:
            d.w = ev
            d.r = {}

    def op(self, e, fn, reads=(), writes=()):
        self._pre(e, reads, writes)
        inst = fn(self.eng[e])
        self.cnt[e] += 1
        inst.then_inc(self.sems[e], 1)
        self._post((e, self.cnt[e]), reads, writes)

    def dma(self, out, in_, reads=(), writes=(), q="sp", is_output=False):
        self._pre(q, reads, writes)
        i = self.dma_rr
        self.dma_rr = (i + 1) % NDMA
        if self.dma_cnt[i] > 0:
            self._wait(q, (("dma", i), 16 * self.dma_cnt[i]))
        self.dma_cnt[i] += 1
        ev = (("dma", i), 16 * self.dma_cnt[i])
        self.eng[q].dma_start(out=out, in_=in_).then_inc(self.dma_sems[i], 16)
        self._post(ev, reads, writes)
        if is_output:
            self.out_events.append(ev)

    def finish(self):
        for i in range(NDMA):
            if self.dma_cnt[i] > 0:
                self._wait("sp", (("dma", i), 16 * self.dma_cnt[i]))


class Ring:
    def __init__(self, tiles):
        self.tiles = tiles
        self.deps = [Dep() for _ in tiles]
        self.i = 0

    def next(self):
        t, d = self.tiles[self.i], self.deps[self.i]
        self.i = (self.i + 1) % len(self.tiles)
        return t, d


class Prog:
    def __init__(self, name):
        self.nc = bass.Bass("TRN2", target_bir_lowering=False)
        self.es = ExitStack()
        self.s = Sched(self.nc, self.es)
        self._n = 0
        self.inputs = []
        self.outputs = []

    def din(self, name, shape, dt=F32):
        self.inputs.append(name)
        return self.nc.dram_tensor(name, list(shape), dt, kind="ExternalInput").ap()

    def dout(self, name, shape, dt=F32):
        self.outputs.append(name)
        return self.nc.dram_tensor(name, list(shape), dt, kind="ExternalOutput").ap()

    def sb(self, shape, dt=F32, name=None):
        self._n += 1
        return self.es.enter_context(self.nc.sbuf_tensor("sb_" + (name or ("t%d" % self._n)), list(shape), dt))

    def ps(self, shape, dt=F32, name=None):
        self._n += 1
        return self.es.enter_context(self.nc.psum_tensor("ps_" + (name or ("p%d" % self._n)), list(shape), dt))

    def ring(self, n, shape, dt=F32, psum=False):
        return Ring([(self.ps if psum else self.sb)(shape, dt) for _ in range(n)])

    def close(self):
        self.s.finish()
        self.es.close()
        return self.nc


class WStream:
    def __init__(self, P, nstage=3, nbf=4, width=2048):
        self.P = P
        self.width = width
        self.stage = P.ring(nstage, [128, width], F32)
        self.wbf = P.ring(nbf, [128, width], BF16)

    def load(self, view, kc, ncols, cast_eng="pool"):
        s = self.P.s
        n = kc * ncols
        assert n <= self.width
        st, sd = self.stage.next()
        wb, wd = self.wbf.next()
        st_v = st[:, 0:n].rearrange("p (k n) -> p k n", k=kc)
        wb_v = wb[:, 0:n].rearrange("p (k n) -> p k n", k=kc)
        s.dma(st_v, view, writes=[sd])
        if cast_eng == "pool":
            s.op("pool", lambda e: e.tensor_copy(out=wb[:, 0:n], in_=st[:, 0:n]), reads=[sd], writes=[wd])
        else:
            s.op(cast_eng, lambda e: e.tensor_copy(out=wb[:, 0:n], in_=st[:, 0:n]), reads=[sd], writes=[wd])
        return wb_v, wd


def rmsnorm_T(P, hT, h_dep, gcol, g_dep, hnT, hn_dep, nk, ntok, ones_f32, ones_dep, psr, sq_ring, tmp_ring, tok0=0):
    s = P.s
    for b0 in range(0, ntok, 512):
        bw = min(512, ntok - b0)
        pt, pd = psr.next()
        for k in range(nk):
            sq, sqd = sq_ring.next()
            s.op("act", lambda e, k=k, sq=sq: e.activation(out=sq[:, 0:bw], in_=hT[:, k, tok0 + b0:tok0 + b0 + bw], func=AF.Square),
                 reads=[h_dep], writes=[sqd])
            s.op("pe", lambda e, k=k, sq=sq: e.matmul(pt[:, 0:bw], lhsT=ones_f32[:, :], rhs=sq[:, 0:bw], start=(k == 0), stop=(k == nk - 1)),
                 reads=[sqd, ones_dep], writes=[pd])
        t1, t1d = tmp_ring.next()
        s.op("act", lambda e: e.activation(out=t1[:, 0:bw], in_=pt[:, 0:bw], func=AF.Sqrt, bias=EPS_AP[0], scale=1.0),
             reads=[pd, EPS_AP[1]], writes=[t1d])
        t2, t2d = tmp_ring.next()
        s.op("dve", lambda e: e.reciprocal(out=t2[:, 0:bw], in_=t1[:, 0:bw]), reads=[t1d], writes=[t2d])
        for k in range(nk):
            s.op("dve", lambda e, k=k: e.scalar_tensor_tensor(out=hnT[:, k, b0:b0 + bw], in0=hT[:, k, tok0 + b0:tok0 + b0 + bw],
                                                             scalar=gcol[:, k:k + 1], in1=t2[:, 0:bw], op0=ALU.mult, op1=ALU.mult),
                 reads=[h_dep, g_dep, t2d], writes=[hn_dep])


EPS_AP = [None, None]


def make_consts(P):
    s = P.s
    eps = P.sb([128, 1], F32, "eps_c")
    epd = Dep()
    s.op("dve", lambda e: e.memset(eps[:, :], EPS), writes=[epd])
    EPS_AP[0] = eps[:, 0:1]
    EPS_AP[1] = epd


def build_tail(dy, final):
    P = Prog("tail")
    s = P.s
    kcy = dy // 128
    hT_in = P.din("hT", [D, TPC])
    yT_in = P.din("yT", [dy, TPC])
    w_out = P.din("w_out", [dy, D])
    gcol_in = P.din("g_ffn", [128, 8])
    wg = P.din("w_gate", [D, DFF])
    wu = P.din("w_up", [D, DFF])
    wd = P.din("w_down", [DFF, D])
    hT_out = P.dout("hT_out", [D, TPC])
    if final:
        gfin_in = P.din("g_fin", [128, 8])
        fin_out = P.dout("fin", [D, TPC])
    make_consts(P)
    TB = 1024
    hT = P.sb([128, 8, TB], F32, "hT")
    h_dep = Dep()
    ybf = P.sb([128, kcy, TB], BF16, "ybf")
    y_dep = Dep()
    hnT = P.sb([128, 8, TB], BF16, "hnT")
    hn_dep = Dep()
    hff = P.sb([128, 22, TB], BF16, "hff")
    hff_deps = [[Dep() for _ in range(2)] for _ in range(22)]
    gcol = P.sb([128, 8], F32, "gcol")
    g_dep = Dep()
    ones = P.sb([128, 128], F32, "ones")
    ones_dep = Dep()
    s.op("dve", lambda e: e.memset(ones[:, :], 1.0 / D), writes=[ones_dep])
    s.dma(gcol[:, :], gcol_in, writes=[g_dep])
    if final:
        gfin = P.sb([128, 8], F32, "gfin")
        gf_dep = Dep()
        s.dma(gfin[:, :], gfin_in, writes=[gf_dep])
        finT = P.sb([128, 8, 512], F32, "finT")
        fin_dep = Dep()
    W = WStream(P)
    psr = P.ring(8, [128, 512], F32, psum=True)
    sq_ring = P.ring(3, [128, 512], F32)
    tmp_ring = P.ring(4, [128, 512], F32)
    ystage = P.ring(2, [128, TB], F32)

    hT_in_v = hT_in.rearrange("(k p) t -> p k t", p=128)
    hT_out_v = hT_out.rearrange("(k p) t -> p k t", p=128)
    yT_in_v = yT_in.rearrange("(k p) t -> p k t", p=128)
    w_out_v = w_out.rearrange("(k p) n -> p k n", p=128)
    wg_v = wg.rearrange("(k p) n -> p k n", p=128)
    wu_v = wu.rearrange("(k p) n -> p k n", p=128)
    wd_v = wd.rearrange("(k p) n -> p k n", p=128)
    if final:
        fin_v = fin_out.rearrange("(k p) t -> p k t", p=128)

    for p in range(TPC // TB):
        t0 = p * TB
        for k in range(8):
            s.dma(hT[:, k, :], hT_in_v[:, k, t0:t0 + TB], writes=[h_dep])
        for k in range(kcy):
            st, sd = ystage.next()
            s.dma(st[:, :], yT_in_v[:, k, t0:t0 + TB], writes=[sd])
            s.op("pool", lambda e, k=k, st=st: e.tensor_copy(out=ybf[:, k, :], in_=st[:, :]), reads=[sd], writes=[y_dep])
        for m in range(8):
            wv, wdep = W.load(w_out_v[:, :, m * 128:(m + 1) * 128], kcy, 128)
            for b in range(TB // 512):
                pt, pd = psr.next()
                for k in range(kcy):
                    s.op("pe", lambda e, k=k: e.matmul(pt[:, :], lhsT=wv[:, k, :], rhs=ybf[:, k, b * 512:(b + 1) * 512],
                                                      start=(k == 0), stop=(k == kcy - 1)), reads=[wdep, y_dep], writes=[pd])
                s.op("dve", lambda e: e.tensor_tensor(out=hT[:, m, b * 512:(b + 1) * 512], in0=hT[:, m, b * 512:(b + 1) * 512],
                                                     in1=pt[:, :], op=ALU.add), reads=[pd, h_dep], writes=[h_dep])
        rmsnorm_T(P, hT, h_dep, gcol, g_dep, hnT, hn_dep, 8, TB, ones, ones_dep, psr, sq_ring, tmp_ring)
        for j2 in range(11):
            gv, gd = W.load(wg_v[:, :, j2 * 256:(j2 + 1) * 256], 8, 256)
            uv, ud = W.load(wu_v[:, :, j2 * 256:(j2 + 1) * 256], 8, 256)
            for jj in range(2):
                j = j2 * 2 + jj
                for b in range(TB // 512):
                    pg, pgd = psr.next()
                    pu, pud = psr.next()
                    for k in range(8):
                        s.op("pe", lambda e, k=k: e.matmul(pg[:, :], lhsT=gv[:, k, jj * 128:(jj + 1) * 128], rhs=hnT[:, k, b * 512:(b + 1) * 512],
                                                          start=(k == 0), stop=(k == 7)), reads=[gd, hn_dep], writes=[pgd])
                    for k in range(8):
                        s.op("pe", lambda e, k=k: e.matmul(pu[:, :], lhsT=uv[:, k, jj * 128:(jj + 1) * 128], rhs=hnT[:, k, b * 512:(b + 1) * 512],
                                                          start=(k == 0), stop=(k == 7)), reads=[ud, hn_dep], writes=[pud])
                    tt, ttd = tmp_ring.next()
                    s.op("act", lambda e: e.activation(out=tt[:, :], in_=pg[:, :], func=AF.Silu), reads=[pgd], writes=[ttd])
                    s.op("dve", lambda e: e.tensor_tensor(out=hff[:, j, b * 512:(b + 1) * 512], in0=tt[:, :], in1=pu[:, :], op=ALU.mult),
                         reads=[ttd, pud], writes=[hff_deps[j][b]])
        for m in range(8):
            dv0, dd0 = W.load(wd_v[:, 0:11, m * 128:(m + 1) * 128], 11, 128)
            dv1, dd1 = W.load(wd_v[:, 11:22, m * 128:(m + 1) * 128], 11, 128)
            for b in range(TB // 512):
                pt, pd = psr.next()
                for j in range(22):
                    wv_, wd_ = (dv0, dd0) if j < 11 else (dv1, dd1)
                    s.op("pe", lambda e, j=j, wv_=wv_: e.matmul(pt[:, :], lhsT=wv_[:, j % 11, :], rhs=hff[:, j, b * 512:(b + 1) * 512],
                                                               start=(j == 0), stop=(j == 21)), reads=[wd_, hff_deps[j][b]], writes=[pd])
                s.op("dve", lambda e: e.tensor_tensor(out=hT[:, m, b * 512:(b + 1) * 512], in0=hT[:, m, b * 512:(b + 1) * 512],
                                                     in1=pt[:, :], op=ALU.add), reads=[pd, h_dep], writes=[h_dep])
        for k in range(8):
            s.dma(hT_out_v[:, k, t0:t0 + TB], hT[:, k, :], reads=[h_dep], is_output=True)
        if final:
            for b in range(TB // 512):
                pt, pd = psr.next()
                for k in range(8):
                    sq, sqd = sq_ring.next()
                    s.op("act", lambda e, k=k, sq=sq: e.activation(out=sq[:, :], in_=hT[:, k, b * 512:(b + 1) * 512], func=AF.Square),
                         reads=[h_dep], writes=[sqd])
                    s.op("pe", lambda e, k=k, sq=sq: e.matmul(pt[:, :], lhsT=ones[:, :], rhs=sq[:, :], start=(k == 0), stop=(k == 7)),
                         reads=[sqd, ones_dep], writes=[pd])
                t1, t1d = tmp_ring.next()
                s.op("act", lambda e: e.activation(out=t1[:, :], in_=pt[:, :], func=AF.Sqrt, bias=EPS_AP[0], scale=1.0),
                     reads=[pd, EPS_AP[1]], writes=[t1d])
                t2, t2d = tmp_ring.next()
                s.op("dve", lambda e: e.reciprocal(out=t2[:, :], in_=t1[:, :]), reads=[t1d], writes=[t2d])
                for k in range(8):
                    s.op("dve", lambda e, k=k: e.scalar_tensor_tensor(out=finT[:, k, :], in0=hT[:, k, b * 512:(b + 1) * 512],
                                                                     scalar=gfin[:, k:k + 1], in1=t2[:, :], op0=ALU.mult, op1=ALU.mult),
                         reads=[h_dep, gf_dep, t2d], writes=[fin_dep])
                for k in range(8):
                    s.dma(fin_v[:, k, t0 + b * 512:t0 + (b + 1) * 512], finT[:, k, :], reads=[fin_dep], is_output=True)
    return P


_CACHE = {}


def get_prog(key, builder):
    if key not in _CACHE:
        P = builder()
        nc = P.close()
        _CACHE[key] = (nc, P)
    return _CACHE[key]


def run(key, builder, in_maps):
    nc, P = get_prog(key, builder)
    res = run_bass_kernel_spmd(nc, in_maps, core_ids=list(range(NCORE)))
    return res.results


def gcols(g):
    return np.ascontiguousarray(g.reshape(-1, 128).T)


def tok_shard_T(a):
    out = []
    for c in range(NCORE):
        b, sg = divmod(c, 4)
        out.append(np.ascontiguousarray(a[b, sg * TPC:(sg + 1) * TPC, :].T))
    return out


def tok_unshard_T(lst):
    F = lst[0].shape[0]
    a = np.empty((NB, S, F), lst[0].dtype)
    for c in range(NCORE):
        b, sg = divmod(c, 4)
        a[b, sg * TPC:(sg + 1) * TPC, :] = lst[c].T
    return a


def run_tail(hT_list, yT_list, w_out, g_ffn, w_gate, w_up, w_down, g_fin=None):
    dy = w_out.shape[0]
    final = g_fin is not None
    maps = []
    for c in range(NCORE):
        m = {"hT": hT_list[c], "yT": yT_list[c], "w_out": np.ascontiguousarray(w_out), "g_ffn": gcols(g_ffn),
             "w_gate": np.ascontiguousarray(w_gate), "w_up": np.ascontiguousarray(w_up), "w_down": np.ascontiguousarray(w_down)}
        if final:
            m["g_fin"] = gcols(g_fin)
        maps.append(m)
    res = run(("tail", dy, final), lambda: build_tail(dy, final), maps)
    return [r["hT_out"] for r in res], ([r["fin"] for r in res] if final else None)


DR = 1280
HALO = 4


def build_lru_a():
    P = Prog("lru_a")
    s = P.s
    TB = 1024
    TE = TB + HALO
    xT_in = P.din("xT_ext", [D, TPC + HALO])
    g_in = P.din("g_mix", [128, 8])
    w_in = P.din("w_in", [D, 2 * DR])
    cw_in = P.din("conv_w", [128, 10, 4])
    vec_in = P.din("vecs", [128, 4, 10])
    wa_in = P.din("w_a", [128, 10, 128])
    wx_in = P.din("w_x", [128, 10, 128])
    gate_out = P.dout("gateT", [DR, TPC])
    hloc_out = P.dout("hlocT", [DR, TPC])
    cuma_out = P.dout("cumAT", [DR, TPC])
    make_consts(P)
    hT = P.sb([128, 8, TE], F32, "hT")
    h_dep = Dep()
    hnT = P.sb([128, 8, TE], BF16, "hnT")
    hn_dep = Dep()
    gcol = P.sb([128, 8], F32, "gcol")
    g_dep = Dep()
    s.dma(gcol[:, :], g_in, writes=[g_dep])
    ones = P.sb([128, 128], F32, "ones")
    ones_dep = Dep()
    s.op("dve", lambda e: e.memset(ones[:, :], 1.0 / D), writes=[ones_dep])
    zeros = P.sb([128, TB], F32, "zeros")
    z_dep = Dep()
    s.op("pool", lambda e: e.memset(zeros[:, :], 0.0), writes=[z_dep])
    cw = P.sb([128, 10, 4], F32, "cw")
    vecs = P.sb([128, 4, 10], F32, "vecs")
    c_dep = Dep()
    s.dma(cw[:, :, :], cw_in, writes=[c_dep])
    s.dma(vecs[:, :, :], vec_in, writes=[c_dep])
    c8 = P.sb([128, 10], F32, "c8")
    c16 = P.sb([128, 10], F32, "c16")
    tmpc = P.sb([128, 10], F32, "tmpc")
    c8_dep = Dep()
    s.op("act", lambda e: e.activation(out=tmpc[:, :], in_=vecs[:, 3, :], func=AF.Exp, scale=-1.0), reads=[c_dep], writes=[c8_dep])
    s.op("act", lambda e: e.activation(out=c8[:, :], in_=tmpc[:, :], func=AF.Ln, bias=ONE_AP[0], scale=1.0), reads=[c8_dep, ONE_AP[1]], writes=[c8_dep])
    s.op("dve", lambda e: e.tensor_scalar(out=c16[:, :], in0=c8[:, :], scalar1=-16.0, scalar2=None, op0=ALU.mult), reads=[c8_dep], writes=[c8_dep])
    s.op("dve", lambda e: e.tensor_scalar(out=c8[:, :], in0=c8[:, :], scalar1=-8.0, scalar2=None, op0=ALU.mult), reads=[c8_dep], writes=[c8_dep])
    wa = P.sb([128, 10, 128], BF16, "wa")
    wx = P.sb([128, 10, 128], BF16, "wx")
    wg_dep = Dep()
    W = WStream(P)
    for src, dst in ((wa_in, wa), (wx_in, wx)):
        st, sd = W.stage.next()
        s.dma(st[:, 0:1280], src.rearrange("p a b -> p (a b)"), writes=[sd])
        s.op("pool", lambda e: e.tensor_copy(out=dst[:, :, :].rearrange("p a b -> p (a b)"), in_=st[:, 0:1280]), reads=[sd], writes=[wg_dep])
    psr = P.ring(8, [128, 512], F32, psum=True)
    sq_ring = P.ring(3, [128, 512], F32)
    tmp_ring = P.ring(4, [128, 512], F32)
    big = P.ring(14, [128, TB], F32)
    xin_ring = P.ring(2, [128, TE], F32)
    xcb_ring = P.ring(2, [128, TB], BF16)
    hprev = [None] * 10
    aprev = [None] * 10

    xT_v = xT_in.rearrange("(k p) t -> p k t", p=128)
    w_in_v = w_in.rearrange("(k p) n -> p k n", p=128)
    gate_v = gate_out.rearrange("(c p) t -> p c t", p=128)
    hloc_v = hloc_out.rearrange("(c p) t -> p c t", p=128)
    cuma_v = cuma_out.rearrange("(c p) t -> p c t", p=128)
    last = P.sb([128, 10, 2, 2], F32, "last")
    last_deps = [[Dep() for _ in range(2)] for _ in range(10)]

    for p in range(TPC // TB):
        t0 = p * TB
        for k in range(8):
            s.dma(hT[:, k, :], xT_v[:, k, t0:t0 + TE], writes=[h_dep])
        rmsnorm_T(P, hT, h_dep, gcol, g_dep, hnT, hn_dep, 8, TE, ones, ones_dep, psr, sq_ring, tmp_ring)
        for c in range(10):
            gv, gd = W.load(w_in_v[:, :, c * 128:(c + 1) * 128], 8, 128)
            xv, xd = W.load(w_in_v[:, :, DR + c * 128:DR + (c + 1) * 128], 8, 128)
            gt, gtd = big.next()
            for b in range(2):
                pt, pd = psr.next()
                c0 = HALO + b * 512
                for k in range(8):
                    s.op("pe", lambda e: e.matmul(pt[:, :], lhsT=gv[:, k, :], rhs=hnT[:, k, c0:c0 + 512], start=(k == 0), stop=(k == 7)),
                         reads=[gd, hn_dep], writes=[pd])
                s.op("act", lambda e: e.activation(out=gt[:, b * 512:(b + 1) * 512], in_=pt[:, :], func=AF.Gelu_apprx_tanh), reads=[pd], writes=[gtd])
            s.dma(gate_v[:, c, t0:t0 + TB], gt[:, :], reads=[gtd], is_output=True)
            xin, xind = xin_ring.next()
            for (c0, bw) in ((0, 512), (512, 512), (1024, HALO)):
                pt, pd = psr.next()
                for k in range(8):
                    s.op("pe", lambda e: e.matmul(pt[:, 0:bw], lhsT=xv[:, k, :], rhs=hnT[:, k, c0:c0 + bw], start=(k == 0), stop=(k == 7)),
                         reads=[xd, hn_dep], writes=[pd])
                s.op("act", lambda e: e.activation(out=xin[:, c0:c0 + bw], in_=pt[:, 0:bw], func=AF.Copy), reads=[pd], writes=[xind])
            xc, xcd = big.next()
            s.op("dve", lambda e: e.tensor_scalar(out=xc[:, :], in0=xin[:, 4:4 + TB], scalar1=cw[:, c, 3:4], scalar2=vecs[:, 0, c:c + 1],
                                                 op0=ALU.mult, op1=ALU.add), reads=[xind, c_dep], writes=[xcd])
            for kk in range(3):
                s.op("dve", lambda e: e.scalar_tensor_tensor(out=xc[:, :], in0=xin[:, 1 + kk:1 + kk + TB], scalar=cw[:, c, kk:kk + 1], in1=xc[:, :],
                                                            op0=ALU.mult, op1=ALU.add), reads=[xind, c_dep, xcd], writes=[xcd])
            xcb, xcbd = xcb_ring.next()
            s.op("pool", lambda e: e.tensor_copy(out=xcb[:, :], in_=xc[:, :]), reads=[xcd], writes=[xcbd])
            rt, rtd = big.next()
            it, itd = big.next()
            for (wt, bi, dst, dd) in ((wa, 1, rt, rtd), (wx, 2, it, itd)):
                for b in range(2):
                    pt, pd = psr.next()
                    s.op("pe", lambda e: e.matmul(pt[:, :], lhsT=wt[:, c, :], rhs=xcb[:, b * 512:(b + 1) * 512], start=True, stop=True),
                         reads=[wg_dep, xcbd], writes=[pd])
                    s.op("act", lambda e: e.activation(out=dst[:, b * 512:(b + 1) * 512], in_=pt[:, :], func=AF.Sigmoid, bias=vecs[:, bi, c:c + 1], scale=1.0),
                         reads=[pd, c_dep], writes=[dd])
            at, atd = big.next()
            a2, a2d = big.next()
            s.op("act", lambda e: e.activation(out=at[:, :], in_=rt[:, :], func=AF.Exp, scale=c8[:, c:c + 1]), reads=[rtd, c8_dep], writes=[atd])
            s.op("act", lambda e: e.activation(out=a2[:, :], in_=rt[:, :], func=AF.Exp, scale=c16[:, c:c + 1]), reads=[rtd, c8_dep], writes=[a2d])
            s.op("dve", lambda e: e.tensor_scalar(out=a2[:, :], in0=a2[:, :], scalar1=-1.0, scalar2=1.0, op0=ALU.mult, op1=ALU.add), reads=[a2d], writes=[a2d])
            s.op("act", lambda e: e.activation(out=a2[:, :], in_=a2[:, :], func=AF.Sqrt), reads=[a2d], writes=[a2d])
            s.op("pool", lambda e: e.tensor_tensor(out=it[:, :], in0=it[:, :], in1=xc[:, :], op=ALU.mult), reads=[itd, xcd], writes=[itd])
            s.op("dve", lambda e: e.tensor_tensor(out=a2[:, :], in0=a2[:, :], in1=it[:, :], op=ALU.mult), reads=[a2d, itd], writes=[a2d])
            hl, hld = big.next()
            ca, cad = big.next()
            ldp = last_deps[c][p]
            if p == 0:
                s.op("dve", lambda e: e.tensor_tensor_scan(out=hl[:, :], data0=at[:, :], data1=a2[:, :], initial=0.0, op0=ALU.mult, op1=ALU.add),
                     reads=[atd, a2d], writes=[hld])
                s.op("dve", lambda e: e.tensor_tensor_scan(out=ca[:, :], data0=at[:, :], data1=zeros[:, :], initial=1.0, op0=ALU.mult, op1=ALU.add),
                     reads=[atd, z_dep], writes=[cad])
            else:
                lp = last_deps[c][p - 1]
                s.op("dve", lambda e: e.tensor_tensor_scan(out=hl[:, :], data0=at[:, :], data1=a2[:, :], initial=last[:, c, p - 1, 0:1], op0=ALU.mult, op1=ALU.add),
                     reads=[atd, a2d, lp], writes=[hld])
                s.op("dve", lambda e: e.tensor_tensor_scan(out=ca[:, :], data0=at[:, :], data1=zeros[:, :], initial=last[:, c, p - 1, 1:2], op0=ALU.mult, op1=ALU.add),
                     reads=[atd, z_dep, lp], writes=[cad])
            s.op("pool", lambda e: e.tensor_copy(out=last[:, c, p, 0:1], in_=hl[:, TB - 1:TB]), reads=[hld], writes=[ldp])
            s.op("pool", lambda e: e.tensor_copy(out=last[:, c, p, 1:2], in_=ca[:, TB - 1:TB]), reads=[cad], writes=[ldp])
            s.dma(hloc_v[:, c, t0:t0 + TB], hl[:, :], reads=[hld], is_output=True)
            s.dma(cuma_v[:, c, t0:t0 + TB], ca[:, :], reads=[cad], is_output=True)
    return P


ONE_AP = [None, None]
_mc0 = make_consts


def make_consts(P):
    _mc0(P)
    s = P.s
    one = P.sb([128, 1], F32, "one_c")
    od = Dep()
    s.op("dve", lambda e: e.memset(one[:, :], 1.0), writes=[od])
    ONE_AP[0] = one[:, 0:1]
    ONE_AP[1] = od


def build_lru_b():
    P = Prog("lru_b")
    s = P.s
    gate_in = P.din("gateT", [DR, TPC])
    hloc_in = P.din("hlocT", [DR, TPC])
    cuma_in = P.din("cumAT", [DR, TPC])
    summ_in = P.din("summ", [128, 4, 2, 10])
    oh_in = P.din("onehot", [128, 4])
    y_out = P.dout("yT", [DR, TPC])
    summ = P.sb([128, 4, 2, 10], F32, "summ")
    oh = P.sb([128, 4], F32, "oh")
    ld = Dep()
    s.dma(summ[:, :, :, :], summ_in, writes=[ld])
    s.dma(oh[:, :], oh_in, writes=[ld])
    st = P.sb([128, 4, 10], F32, "st")
    hin = P.sb([128, 10], F32, "hin")
    cd = Dep()
    s.op("dve", lambda e: e.memset(st[:, :, :], 0.0), writes=[cd])
    for k in range(3):
        s.op("dve", lambda e: e.tensor_tensor(out=st[:, k + 1, :], in0=summ[:, k, 1, :], in1=st[:, k, :], op=ALU.mult), reads=[ld, cd], writes=[cd])
        s.op("dve", lambda e: e.tensor_tensor(out=st[:, k + 1, :], in0=st[:, k + 1, :], in1=summ[:, k, 0, :], op=ALU.add), reads=[ld, cd], writes=[cd])
    s.op("dve", lambda e: e.tensor_scalar(out=hin[:, :], in0=st[:, 0, :], scalar1=oh[:, 0:1], scalar2=None, op0=ALU.mult), reads=[ld, cd], writes=[cd])
    for k in range(1, 4):
        s.op("dve", lambda e: e.scalar_tensor_tensor(out=hin[:, :], in0=st[:, k, :], scalar=oh[:, k:k + 1], in1=hin[:, :], op0=ALU.mult, op1=ALU.add),
             reads=[ld, cd], writes=[cd])
    ring = P.ring(9, [128, TPC], F32)
    g_v = gate_in.rearrange("(c p) t -> p c t", p=128)
    h_v = hloc_in.rearrange("(c p) t -> p c t", p=128)
    a_v = cuma_in.rearrange("(c p) t -> p c t", p=128)
    y_v = y_out.rearrange("(c p) t -> p c t", p=128)
    for c in range(10):
        gt, gd = ring.next()
        ht, hd = ring.next()
        at, ad = ring.next()
        s.dma(gt[:, :], g_v[:, c, :], writes=[gd])
        s.dma(ht[:, :], h_v[:, c, :], writes=[hd])
        s.dma(at[:, :], a_v[:, c, :], writes=[ad])
        s.op("dve", lambda e: e.scalar_tensor_tensor(out=ht[:, :], in0=at[:, :], scalar=hin[:, c:c + 1], in1=ht[:, :], op0=ALU.mult, op1=ALU.add),
             reads=[ad, hd, cd], writes=[hd])
        s.op("pool", lambda e: e.tensor_tensor(out=gt[:, :], in0=gt[:, :], in1=ht[:, :], op=ALU.mult), reads=[hd, gd], writes=[gd])
        s.dma(y_v[:, c, :], gt[:, :], reads=[gd], is_output=True)
    return P


def run_lru(h, g_mix, w_in, conv_w, conv_b, w_a, b_a, w_x, b_x, lam):
    maps = []
    cw = np.ascontiguousarray(conv_w.T.reshape(10, 128, 4).transpose(1, 0, 2))
    vecs = np.ascontiguousarray(np.stack([conv_b, b_a, b_x, lam]).reshape(4, 10, 128).transpose(2, 0, 1))
    wa = np.ascontiguousarray(w_a.transpose(1, 0, 2))
    wx = np.ascontiguousarray(w_x.transpose(1, 0, 2))
    for c in range(NCORE):
        b, sg = divmod(c, 4)
        ext = np.zeros((D, TPC + HALO), np.float32)
        ext[:, HALO:] = h[c]
        if sg > 0:
            ext[:, :HALO] = h[c - 1][:, TPC - HALO:]
        maps.append({"xT_ext": ext, "g_mix": gcols(g_mix), "w_in": np.ascontiguousarray(w_in), "conv_w": cw, "vecs": vecs, "w_a": wa, "w_x": wx})
    ra = run(("lru_a",), build_lru_a, maps)
    maps = []
    for c in range(NCORE):
        b, sg = divmod(c, 4)
        summ = np.empty((128, 4, 2, 10), np.float32)
        for k in range(4):
            r = ra[b * 4 + k]
            summ[:, k, 0, :] = r["hlocT"][:, -1].reshape(10, 128).T
            summ[:, k, 1, :] = r["cumAT"][:, -1].reshape(10, 128).T
        oh = np.zeros((128, 4), np.float32)
        oh[:, sg] = 1.0
        maps.append({"gateT": ra[c]["gateT"], "hlocT": ra[c]["hlocT"], "cumAT": ra[c]["cumAT"], "summ": summ, "onehot": oh})
    rb = run(("lru_b",), build_lru_b, maps)
    return [r["yT"] for r in rb]


def build_proj(nout):
    P = Prog("proj")
    s = P.s
    TB = 1024
    hT_in = P.din("hT", [D, TPC])
    g_in = P.din("g_mix", [128, 8])
    w_in = P.din("w_in", [D, nout])
    out = P.dout("projT", [nout, TPC])
    make_consts(P)
    hT = P.sb([128, 8, TB], F32, "hT")
    h_dep = Dep()
    hnT = P.sb([128, 8, TB], BF16, "hnT")
    hn_dep = Dep()
    gcol = P.sb([128, 8], F32, "gcol")
    g_dep = Dep()
    s.dma(gcol[:, :], g_in, writes=[g_dep])
    ones = P.sb([128, 128], F32, "ones")
    ones_dep = Dep()
    s.op("dve", lambda e: e.memset(ones[:, :], 1.0 / D), writes=[ones_dep])
    W = WStream(P)
    psr = P.ring(8, [128, 512], F32, psum=True)
    sq_ring = P.ring(3, [128, 512], F32)
    tmp_ring = P.ring(4, [128, 512], F32)
    oring = P.ring(4, [128, TB], F32)
    hT_v = hT_in.rearrange("(k p) t -> p k t", p=128)
    w_v = w_in.rearrange("(k p) n -> p k n", p=128)
    nm = (nout + 127) // 128
    for p in range(TPC // TB):
        t0 = p * TB
        for k in range(8):
            s.dma(hT[:, k, :], hT_v[:, k, t0:t0 + TB], writes=[h_dep])
        rmsnorm_T(P, hT, h_dep, gcol, g_dep, hnT, hn_dep, 8, TB, ones, ones_dep, psr, sq_ring, tmp_ring)
        for m in range(nm):
            mw = min(128, nout - m * 128)
            wv, wd = W.load(w_v[:, :, m * 128:m * 128 + mw], 8, mw)
            ot, od = oring.next()
            for b in range(TB // 512):
                pt, pd = psr.next()
                for k in range(8):
                    s.op("pe", lambda e: e.matmul(pt[0:mw, :], lhsT=wv[:, k, :], rhs=hnT[:, k, b * 512:(b + 1) * 512], start=(k == 0), stop=(k == 7)),
                         reads=[wd, hn_dep], writes=[pd])
                if b % 2 == 0:
                    s.op("act", lambda e: e.activation(out=ot[0:mw, b * 512:(b + 1) * 512], in_=pt[0:mw, :], func=AF.Copy), reads=[pd], writes=[od])
                else:
                    s.op("dve", lambda e: e.tensor_copy(out=ot[0:mw, b * 512:(b + 1) * 512], in_=pt[0:mw, :]), reads=[pd], writes=[od])
            s.dma(out[m * 128:m * 128 + mw, t0:t0 + TB], ot[0:mw, :], reads=[od], is_output=True)
    return P


def run_proj(hl, g_mix, w_in):
    nout = w_in.shape[1]
    maps = [{"hT": hl[c], "g_mix": gcols(g_mix), "w_in": np.ascontiguousarray(w_in)} for c in range(NCORE)]
    res = run(("proj", nout), lambda: build_proj(nout), maps)
    return [r["projT"] for r in res]


def assemble_T(lst):
    F = lst[0].shape[0]
    a = np.empty((NB, F, S), lst[0].dtype)
    for c in range(NCORE):
        b, sg = divmod(c, 4)
        a[b, :, sg * TPC:(sg + 1) * TPC] = lst[c]
    return a


GC = 128
SBK = 2048


def build_gla_b():
    P = Prog("gla_b")
    s = P.s
    nsb = S // SBK
    ncs = SBK // GC
    qT_in = P.din("qT", [128, S])
    kT_in = P.din("kT", [128, S])
    gl_in = P.din("glowT", [16, S])
    wg2_in = P.din("w_g2", [16, 128])
    bg2_in = P.din("b_g2", [128, 1])
    v_in = P.din("v", [128, S // GC, 256])
    r_in = P.din("r", [128, S // GC, 256])
    gb_in = P.din("gnorm", [128, 256])
    id_in = P.din("ident", [128, 128])
    cm_in = P.din("cmask", [128, 128])
    rm_in = P.din("rmask", [128, SBK])
    y_out = P.dout("y", [128, S // GC, 256])
    make_consts(P)
    cd = Dep()
    ident = P.sb([128, 128], BF16, "ident")
    cmask = P.sb([128, 128], F32, "cmask")
    rmask = P.sb([128, SBK], F32, "rmask")
    gb = P.sb([128, 256], F32, "gb")
    negb = P.sb([128, 1], F32, "negb")
    wg2 = P.sb([16, 128], BF16, "wg2")
    stg = P.ring(6, [128, SBK], F32)
    t_, td_ = stg.next()
    s.dma(t_[:, 0:128], id_in, writes=[td_])
    s.op("pool", lambda e: e.tensor_copy(out=ident[:, :], in_=t_[:, 0:128]), reads=[td_], writes=[cd])
    t2_, td2_ = stg.next()
    s.dma(t2_[0:16, 0:128], wg2_in, writes=[td2_])
    s.op("pool", lambda e: e.tensor_copy(out=wg2[:, :], in_=t2_[0:16, 0:128]), reads=[td2_], writes=[cd])
    s.dma(cmask[:, :], cm_in, writes=[cd])
    s.dma(rmask[:, :], rm_in, writes=[cd])
    s.dma(gb[:, :], gb_in, writes=[cd])
    s.dma(negb[:, :], bg2_in, writes=[cd])
    s.op("dve", lambda e: e.tensor_scalar(out=negb[:, :], in0=negb[:, :], scalar1=-1.0, scalar2=None, op0=ALU.mult), reads=[cd], writes=[cd])
    state = P.sb([128, 256], F32, "state")
    st_dep = Dep()
    s.op("dve", lambda e: e.memset(state[:, :], 0.0), writes=[st_dep])
    sbf = P.ring(2, [128, 256], BF16)
    sb_cur, sb_cur_d = sbf.next()
    s.op("act", lambda e: e.activation(out=sb_cur[:, :], in_=state[:, :], func=AF.Copy), reads=[st_dep], writes=[sb_cur_d])
    psr = P.ring(5, [128, 512], F32, psum=True)
    pstr = P.ring(2, [128, 128], BF16, psum=True)
    glbf = P.sb([16, SBK], BF16, "glbf")
    glbf_d = Dep()
    qdec = P.ring(2, [128, SBK], BF16)
    kinv = P.ring(2, [128, SBK], BF16)
    kend = P.ring(2, [128, SBK], BF16)
    vbf = P.ring(2, [128, ncs, 256], BF16)
    dec_r = P.ring(2, [128, ncs], F32)
    att_r = P.ring(3, [128, 128], BF16)
    ket_r = P.ring(3, [128, 128], BF16)
    sm_r = P.ring(6, [128, 1], F32)
    y1_r = P.ring(3, [128, 256], F32)
    junk = P.ring(2, [128, 256], F32)
    yt_r = P.ring(2, [128, ncs, 256], F32)
    v_st_r = P.ring(2, [128, ncs, 256], F32)
    r_st_r = P.ring(2, [128, ncs, 256], F32)

    for sbi in range(nsb):
        t0 = sbi * SBK
        q_st, q_d = stg.next()
        k_st, k_d = stg.next()
        g_st, g_d = stg.next()
        s.dma(q_st[:, :], qT_in[:, t0:t0 + SBK], writes=[q_d])
        s.dma(k_st[:, :], kT_in[:, t0:t0 + SBK], writes=[k_d])
        s.dma(g_st[0:16, :], gl_in[:, t0:t0 + SBK], writes=[g_d])
        v_st, v_d = v_st_r.next()
        r_st, r_d = r_st_r.next()
        s.dma(v_st[:, :, :], v_in[:, sbi * ncs:(sbi + 1) * ncs, :], writes=[v_d])
        s.dma(r_st[:, :, :], r_in[:, sbi * ncs:(sbi + 1) * ncs, :], writes=[r_d])
        s.op("pool", lambda e: e.tensor_copy(out=glbf[:, :], in_=g_st[0:16, :]), reads=[g_d], writes=[glbf_d])
        e1, e1d = stg.next()
        for b in range(SBK // 512):
            pt, pd = psr.next()
            s.op("pe", lambda e: e.matmul(pt[:, :], lhsT=wg2[:, :], rhs=glbf[:, b * 512:(b + 1) * 512], start=True, stop=True),
                 reads=[cd, glbf_d], writes=[pd])
            s.op("act", lambda e: e.activation(out=e1[:, b * 512:(b + 1) * 512], in_=pt[:, :], func=AF.Exp, bias=negb[:, 0:1], scale=-1.0),
                 reads=[pd, cd], writes=[e1d])
        s.op("act", lambda e: e.activation(out=e1[:, :], in_=e1[:, :], func=AF.Ln, bias=ONE_AP[0], scale=1.0), reads=[e1d, ONE_AP[1]], writes=[e1d])
        s.op("dve", lambda e: e.tensor_scalar(out=e1[:, :], in0=e1[:, :], scalar1=-1.0 / 16.0, scalar2=None, op0=ALU.mult), reads=[e1d], writes=[e1d])
        bT, bd = stg.next()
        s.op("dve", lambda e: e.tensor_tensor_scan(out=bT[:, :], data0=rmask[:, :], data1=e1[:, :], initial=0.0, op0=ALU.mult, op1=ALU.add),
             reads=[cd, e1d], writes=[bd])
        s.op("act", lambda e: e.activation(out=e1[:, :], in_=bT[:, :], func=AF.Exp), reads=[bd], writes=[e1d])
        s.op("act", lambda e: e.activation(out=g_st[:, :], in_=bT[:, :], func=AF.Exp, scale=-1.0), reads=[bd], writes=[g_d])
        dec, decd = dec_r.next()
        s.op("dve", lambda e: e.tensor_copy(out=dec[:, :], in_=e1[:, :].rearrange("p (n c) -> p n c", c=GC)[:, :, GC - 1]), reads=[e1d], writes=[decd])
        qd, qdd = qdec.next()
        s.op("dve", lambda e: e.scalar_tensor_tensor(out=qd[:, :], in0=q_st[:, :], scalar=float(128 ** -0.5), in1=e1[:, :], op0=ALU.mult, op1=ALU.mult),
             reads=[q_d, e1d], writes=[qdd])
        s.op("pool", lambda e: e.tensor_tensor(out=k_st[:, :], in0=k_st[:, :], in1=g_st[:, :], op=ALU.mult), reads=[k_d, g_d], writes=[k_d])
        ki, kid = kinv.next()
        s.op("pool", lambda e: e.tensor_copy(out=ki[:, :], in_=k_st[:, :]), reads=[k_d], writes=[kid])
        ke, ked = kend.next()
        s.op("dve", lambda e: e.tensor_tensor(out=ke[:, :].rearrange("p (n c) -> p n c", c=GC), in0=k_st[:, :].rearrange("p (n c) -> p n c", c=GC),
                                             in1=dec[:, :].rearrange("p (n o) -> p n o", o=1).to_broadcast([128, ncs, GC]), op=ALU.mult),
             reads=[k_d, decd], writes=[ked])
        vb, vbd = vbf.next()
        s.op("pool", lambda e: e.tensor_copy(out=vb[:, :, :], in_=v_st[:, :, :]), reads=[v_d], writes=[vbd])
        s.op("act", lambda e: e.activation(out=r_st[:, :, :], in_=r_st[:, :, :], func=AF.Silu), reads=[r_d], writes=[r_d])
        yt, ytd = yt_r.next()
        for n in range(ncs):
            c0 = n * GC
            pa, pad = psr.next()
            s.op("pe", lambda e: e.matmul(pa[:, 0:128], lhsT=ki[:, c0:c0 + GC], rhs=qd[:, c0:c0 + GC], start=True, stop=True), reads=[kid, qdd], writes=[pad])
            att, attd = att_r.next()
            s.op("dve", lambda e: e.tensor_tensor(out=att[:, :], in0=pa[:, 0:128], in1=cmask[:, :], op=ALU.mult), reads=[pad, cd], writes=[attd])
            po, pod = psr.next()
            s.op("pe", lambda e: e.matmul(po[:, 0:256], lhsT=att[:, :], rhs=vb[:, n, :], start=True, stop=False), reads=[attd, vbd], writes=[pod])
            s.op("pe", lambda e: e.matmul(po[:, 0:256], lhsT=qd[:, c0:c0 + GC], rhs=sb_cur[:, :], start=False, stop=True), reads=[qdd, sb_cur_d], writes=[pod])
            ptr, ptrd = pstr.next()
            s.op("pe", lambda e: e.transpose(ptr[:, :], ke[:, c0:c0 + GC], ident[:, :]), reads=[ked, cd], writes=[ptrd])
            ket, ketd = ket_r.next()
            s.op("act", lambda e: e.activation(out=ket[:, :], in_=ptr[:, :], func=AF.Copy), reads=[ptrd], writes=[ketd])
            pds, pdsd = psr.next()
            s.op("pe", lambda e: e.matmul(pds[:, 0:256], lhsT=ket[:, :], rhs=vb[:, n, :], start=True, stop=True), reads=[ketd, vbd], writes=[pdsd])
            s.op("dve", lambda e: e.scalar_tensor_tensor(out=state[:, :], in0=state[:, :], scalar=dec[:, n:n + 1], in1=pds[:, 0:256], op0=ALU.mult, op1=ALU.add),
                 reads=[decd, pdsd, st_dep], writes=[st_dep])
            sb_cur, sb_cur_d = sbf.next()
            s.op("act", lambda e: e.activation(out=sb_cur[:, :], in_=state[:, :], func=AF.Copy), reads=[st_dep], writes=[sb_cur_d])
            jk, jkd = junk.next()
            ss, ssd = sm_r.next()
            s.op("act", lambda e: e.activation(out=jk[:, :], in_=po[:, 0:256], func=AF.Square, accum_out=ss[:, 0:1]), reads=[pod], writes=[jkd, ssd])
            s.op("dve", lambda e: e.tensor_scalar(out=ss[:, :], in0=ss[:, :], scalar1=1.0 / 256.0, scalar2=EPS, op0=ALU.mult, op1=ALU.add), reads=[ssd], writes=[ssd])
            s.op("act", lambda e: e.activation(out=ss[:, :], in_=ss[:, :], func=AF.Sqrt), reads=[ssd], writes=[ssd])
            s.op("dve", lambda e: e.reciprocal(out=ss[:, :], in_=ss[:, :]), reads=[ssd], writes=[ssd])
            y1, y1d = y1_r.next()
            s.op("dve", lambda e: e.scalar_tensor_tensor(out=y1[:, :], in0=po[:, 0:256], scalar=ss[:, 0:1], in1=gb[:, :], op0=ALU.mult, op1=ALU.mult),
                 reads=[pod, ssd, cd], writes=[y1d])
            s.op("pool", lambda e: e.tensor_tensor(out=yt[:, n, :], in0=y1[:, :], in1=r_st[:, n, :], op=ALU.mult), reads=[y1d, r_d], writes=[ytd])
        s.dma(y_out[:, sbi * ncs:(sbi + 1) * ncs, :], yt[:, :, :], reads=[ytd], is_output=True)
    return P


def run_gla(h, g_mix, w_in, w_g2, b_g2, gnorm):
    projT = assemble_T(run_proj(h, g_mix, w_in))
    ident = np.eye(128, dtype=np.float32)
    cmask = np.triu(np.ones((128, 128), np.float32))
    rmask = np.ones((128, SBK), np.float32)
    rmask[:, ::GC] = 0.0
    maps = []
    for c in range(NCORE):
        b, hd = divmod(c, 4)
        pt = projT[b]
        v = pt[1024 + hd * 256:1024 + (hd + 1) * 256, :].T.reshape(S // GC, GC, 256).transpose(1, 0, 2)
        r = pt[2048 + hd * 256:2048 + (hd + 1) * 256, :].T.reshape(S // GC, GC, 256).transpose(1, 0, 2)
        maps.append({"qT": np.ascontiguousarray(pt[hd * 128:(hd + 1) * 128]), "kT": np.ascontiguousarray(pt[512 + hd * 128:512 + (hd + 1) * 128]),
                     "glowT": np.ascontiguousarray(pt[3072:3088]), "w_g2": np.ascontiguousarray(w_g2[:, hd * 128:(hd + 1) * 128]),
                     "b_g2": np.ascontiguousarray(b_g2[hd * 128:(hd + 1) * 128].reshape(128, 1)),
                     "v": np.ascontiguousarray(v), "r": np.ascontiguousarray(r), "gnorm": np.ascontiguousarray(np.broadcast_to(gnorm, (128, 256))),
                     "ident": ident, "cmask": cmask, "rmask": rmask})
    res = run(("gla_b",), build_gla_b, maps)
    y = np.empty((NB, S, 1024), np.float32)
    for c in range(NCORE):
        b, hd = divmod(c, 4)
        y[b, :, hd * 256:(hd + 1) * 256] = res[c]["y"].transpose(1, 0, 2).reshape(S, 256)
    return tok_shard_T(y)


NEG = -30000.0
QC = 512
NQC = S // QC
NKT = S // 128


def rel_bucket_np(dist):
    n = np.maximum(dist, 0)
    nf = np.maximum(n, 1).astype(np.float32)
    large = 16 + (np.log(nf / np.float32(16)) / np.float32(np.log(8.0)) * np.float32(16)).astype(np.int32)
    large = np.minimum(large, 31)
    return np.where(n < 16, n, large).astype(np.int64)


def load_cast(P, ring, dst2d, src2d, npart, n, dep, width=1024, eng="pool"):
    s = P.s
    for c0 in range(0, n, width):
        w = min(width, n - c0)
        st, sd = ring.next()
        s.dma(st[0:npart, 0:w], src2d[:, c0:c0 + w], writes=[sd])
        s.op(eng, lambda e: e.tensor_copy(out=dst2d[:, c0:c0 + w], in_=st[0:npart, 0:w]), reads=[sd], writes=[dep])


def build_nsa_b():
    P = Prog("nsa_b")
    s = P.s
    q_in = P.din("q", [64, 4, S])
    kvT_in = P.din("kvT", [64, 4, S])
    vtok_in = P.din("vtok", [128, 2, NKT, 64])
    gates_in = P.din("gates", [128, NKT, 12])
    cpos_in = P.din("cpos", [64, 2, 32])
    cw1_in = P.din("cw1", [64, 2, 32, 64])
    cw2_in = P.din("cw2", [64, 2, 64])
    cb_in = P.din("cb", [64, 2, 2])
    bsel_in = P.din("bias_sel", [128, 4, 5, 512])
    bw1_in = P.din("bias_w1", [128, 4, 512])
    mw_in = P.din("mask_w", [128, 3, 512])
    bcmp_in = P.din("bias_cmp", [128, 4, 5, 512])
    ch_in = P.din("ch", [128, 4])
    E_in = P.din("E", [128, NKT, 128])
    selmap_in = P.din("selmap", [128, 4, 128])
    id_in = P.din("ident", [128, 128])
    am_in = P.din("addmask", [128, NKT, 128])
    o_out = P.dout("o", [128, NKT, 256])
    make_consts(P)
    stage = P.ring(3, [128, 1024], F32)
    cd = Dep()
    bsel = P.sb([128, 4, 5, 512], BF16, "bsel")
    bw1 = P.sb([128, 4, 512], BF16, "bw1")
    mw = P.sb([128, 3, 512], BF16, "mw")
    bcmp = P.sb([128, 4, 5, 512], BF16, "bcmp")
    E = P.sb([128, NKT, 128], BF16, "E")
    selm = P.sb([128, 4, 128], BF16, "selm")
    ident = P.sb([128, 128], BF16, "ident")
    ch = P.sb([128, 4], F32, "ch")
    s.dma(ch[:, :], ch_in, writes=[cd])
    load_cast(P, stage, bsel[:, :, :, :].rearrange("p a b c -> p (a b c)"), bsel_in.rearrange("p a b c -> p (a b c)"), 128, 4 * 5 * 512, cd)
    load_cast(P, stage, bw1[:, :, :].rearrange("p a c -> p (a c)"), bw1_in.rearrange("p a c -> p (a c)"), 128, 4 * 512, cd)
    load_cast(P, stage, mw[:, :, :].rearrange("p a c -> p (a c)"), mw_in.rearrange("p a c -> p (a c)"), 128, 3 * 512, cd)
    load_cast(P, stage, bcmp[:, :, :, :].rearrange("p a b c -> p (a b c)"), bcmp_in.rearrange("p a b c -> p (a b c)"), 128, 4 * 5 * 512, cd)
    load_cast(P, stage, E[:, :, :].rearrange("p a c -> p (a c)"), E_in.rearrange("p a c -> p (a c)"), 128, NKT * 128, cd)
    load_cast(P, stage, selm[:, :, :].rearrange("p a c -> p (a c)"), selmap_in.rearrange("p a c -> p (a c)"), 128, 512, cd)
    load_cast(P, stage, ident[:, :], id_in, 128, 128, cd)
    ks = P.sb([64, S], BF16, "ks")
    kw = P.sb([64, S], BF16, "kw")
    craw = P.sb([64, S], BF16, "craw")
    kd = Dep()
    craw_d = Dep()
    load_cast(P, stage, ks[:, :], kvT_in[:, 2, :], 64, S, kd)
    load_cast(P, stage, kw[:, :], kvT_in[:, 3, :], 64, S, kd)
    vs = P.sb([128, NKT, 65], BF16, "vs")
    vw = P.sb([128, NKT, 65], BF16, "vw")
    vd = Dep()
    s.op("dve", lambda e: e.memset(vs[:, :, 64:65], 1.0), writes=[vd])
    s.op("dve", lambda e: e.memset(vw[:, :, 64:65], 1.0), writes=[vd])
    for j, dst in ((0, vs), (1, vw)):
        for c0 in range(0, NKT, 16):
            st, sd = stage.next()
            s.dma(st[:, 0:1024].rearrange("p (a c) -> p a c", c=64), vtok_in[:, j, c0:c0 + 16, :], writes=[sd])
            s.op("pool", lambda e: e.tensor_copy(out=dst[:, c0:c0 + 16, 0:64], in_=st[:, 0:1024].rearrange("p (a c) -> p a c", c=64)), reads=[sd], writes=[vd])
    gates = P.sb([128, NKT, 12], F32, "gates")
    gd = Dep()
    s.dma(gates[:, :, :], gates_in, writes=[gd])
    s.op("act", lambda e: e.activation(out=gates[:, :, :], in_=gates[:, :, :], func=AF.Sigmoid), reads=[gd], writes=[gd])
    psS = P.ring(3, [128, 512], F32, psum=True)
    psA = [P.ps([128, 512], F32) for _ in range(4)]
    psA_d = [Dep() for _ in range(4)]
    _ptb = P.ps([128, 1024], BF16)
    psT = Ring([_ptb[:, 0:512], _ptb[:, 512:1024]])
    cpos = P.sb([64, 2, 32], BF16, "cpos")
    cw1 = P.sb([64, 2, 32, 64], BF16, "cw1")
    cw2 = P.sb([64, 2, 64], BF16, "cw2")
    cb = P.sb([64, 2, 2], F32, "cb")
    cpd = Dep()
    s.dma(cb[:, :, :], cb_in, writes=[cpd])
    load_cast(P, stage, cpos[:, :, :].rearrange("p a c -> p (a c)"), cpos_in.rearrange("p a c -> p (a c)"), 64, 64, cpd)
    load_cast(P, stage, cw1[:, :, :, :].rearrange("p a b c -> p (a b c)"), cw1_in.rearrange("p a b c -> p (a b c)"), 64, 2 * 32 * 64, cpd)
    load_cast(P, stage, cw2[:, :, :].rearrange("p a c -> p (a c)"), cw2_in.rearrange("p a c -> p (a c)"), 64, 128, cpd)
    kcT = P.sb([64, 512], BF16, "kcT")
    vcT = P.sb([64, 512], BF16, "vcT")
    vce = P.sb([128, 4, 193], BF16, "vce")
    kcd = Dep()
    hb = P.sb([64, 2], F32, "hb")
    hid = P.sb([64, 512], BF16, "hid")
    hid_d = Dep()
    s.op("dve", lambda e: e.memset(kcT[:, :], 0.0), writes=[kcd])
    s.op("dve", lambda e: e.memset(vcT[:, :], 0.0), writes=[kcd])
    s.op("dve", lambda e: e.memset(vce[:, :, 64:65], 1.0), writes=[kcd])
    s.op("dve", lambda e: e.memset(vce[127:128, 3, 64:65], 0.0), writes=[kcd]) if False else None
    for j, dstT in ((0, kcT), (1, vcT)):
        load_cast(P, stage, craw[:, :], kvT_in[:, j, :], 64, S, craw_d)
        pc, pcd = psS.next()
        for l in range(32):
            s.op("pe", lambda e: e.matmul(pc[0:64, 0:1], lhsT=cw1[:, j, l, :], rhs=cpos[:, j, l:l + 1], start=(l == 0), stop=(l == 31)), reads=[cpd], writes=[pcd])
        s.op("dve", lambda e: e.tensor_tensor(out=hb[:, j:j + 1], in0=pc[0:64, 0:1], in1=cb[:, j, 0:1], op=ALU.add), reads=[pcd, cpd], writes=[hid_d])
        ph, phd = psS.next()
        crv = craw[:, :].rearrange("p (n c) -> p n c", c=16)
        for l in range(32):
            s.op("pe", lambda e: e.matmul(ph[0:64, 0:511], lhsT=cw1[:, j, l, :], rhs=crv[:, (l // 16):(l // 16) + 511, l % 16], start=(l == 0), stop=(l == 31)),
                 reads=[cpd, craw_d], writes=[phd])
        s.op("act", lambda e: e.activation(out=hid[:, 0:511], in_=ph[0:64, 0:511], func=AF.Gelu_apprx_tanh, bias=hb[:, j:j + 1], scale=1.0), reads=[phd, hid_d], writes=[hid_d])
        p2, p2d = psS.next()
        s.op("pe", lambda e: e.matmul(p2[0:64, 0:511], lhsT=cw2[:, j, :], rhs=hid[:, 0:511], start=True, stop=True), reads=[cpd, hid_d], writes=[p2d])
        s.op("act", lambda e: e.activation(out=dstT[:, 0:511], in_=p2[0:64, 0:511], func=AF.Identity, bias=cb[:, j, 1:2], scale=1.0), reads=[p2d, cpd], writes=[kcd])
    for ct in range(4):
        pt, ptd = psT.next()
        s.op("pe", lambda e: e.transpose(pt[:, 0:64], vcT[:, ct * 128:(ct + 1) * 128], ident[0:64, 0:64]), reads=[kcd, cd], writes=[ptd])
        s.op("act", lambda e: e.activation(out=vce[:, ct, 0:64], in_=pt[:, 0:64], func=AF.Copy), reads=[ptd], writes=[kcd])
    s.op("pool", lambda e: e.tensor_copy(out=vce[:, :, 65:193], in_=selm[:, :, :]), reads=[cd], writes=[kcd])
    zt = P.sb([128, 4, 1], BF16, "zt")
    s.op("dve", lambda e: e.memset(zt[:, :, :], 1.0), writes=[kcd])
    s.dma(vce[127:128, 3, 64:65], zt[127:128, 0, 0:1], reads=[kcd], writes=[kcd]) if False else None

    qst_r = P.ring(2, [64, 4, QC], F32)
    qb_r = P.ring(2, [64, 4, QC], BF16)
    pT_r = P.ring(6, [128, 512], BF16)
    nmT_r = P.ring(2, [128, 512], BF16)
    am_r = P.ring(2, [128, 4, 128], F32)
    imp_r = P.ring(2, [128, 4, 128], F32)
    sc_r = P.ring(2, [128, 128], F32)
    m8_r = P.ring(4, [128, 8], F32)
    nm_r = P.ring(2, [128, 128], BF16)
    c1_r = P.ring(8, [128, 1], F32)
    oacc_r = P.ring(2, [128, 4, 256], F32)

    def finish_branch(acc, accd, hg, sub, br, qt, oacc, oaccd, first, imp=None, impd=None, imp_first=False):
        c1, c1d = c1_r.next()
        s.op("dve", lambda e: e.tensor_scalar(out=c1[:, :], in0=acc[:, 64:65], scalar1=1e-30, scalar2=None, op0=ALU.max), reads=[accd], writes=[c1d])
        s.op("dve", lambda e: e.reciprocal(out=c1[:, :], in_=c1[:, :]), reads=[c1d], writes=[c1d])
        if imp is not None:
            if imp_first:
                s.op("dve", lambda e: e.tensor_scalar(out=imp[:, sub, :], in0=acc[:, 65:193], scalar1=c1[:, 0:1], scalar2=None, op0=ALU.mult), reads=[accd, c1d], writes=[impd])
            else:
                s.op("dve", lambda e: e.scalar_tensor_tensor(out=imp[:, sub, :], in0=acc[:, 65:193], scalar=c1[:, 0:1], in1=imp[:, sub, :], op0=ALU.mult, op1=ALU.add),
                     reads=[accd, c1d, impd], writes=[impd])
        s.op("dve", lambda e: e.tensor_tensor(out=c1[:, :], in0=c1[:, :], in1=gates[:, qt, hg * 3 + br:hg * 3 + br + 1], op=ALU.mult), reads=[c1d, gd], writes=[c1d])
        dst = oacc[:, sub, hg * 64:(hg + 1) * 64]
        if first:
            s.op("dve", lambda e: e.tensor_scalar(out=dst, in0=acc[:, 0:64], scalar1=c1[:, 0:1], scalar2=None, op0=ALU.mult), reads=[accd, c1d], writes=[oaccd])
        else:
            s.op("dve", lambda e: e.scalar_tensor_tensor(out=dst, in0=acc[:, 0:64], scalar=c1[:, 0:1], in1=dst, op0=ALU.mult, op1=ALU.add), reads=[accd, c1d, oaccd], writes=[oaccd])

    for qc in range(NQC):
        t0 = qc * QC
        qst, qsd = qst_r.next()
        qb, qbd = qb_r.next()
        s.dma(qst[:, :, :], q_in[:, :, t0:t0 + QC], writes=[qsd])
        s.op("pool", lambda e: e.tensor_scalar(out=qb[:, :, :], in0=qst[:, :, :], scalar1=0.125, scalar2=None, op0=ALU.mult), reads=[qsd], writes=[qbd])
        am, amd = am_r.next()
        s.dma(am[:, :, :], am_in[:, qc * 4:(qc + 1) * 4, :], writes=[amd])
        oacc, oaccd = oacc_r.next()
        imp, impd = imp_r.next()
        nct = min(4, (32 * qc + 30) // 128 + 1)
        for hg in range(4):
            pts = []
            for ct in range(nct):
                e_ = (QC * qc - 2048 * ct) // 512
                pS, pSd = psS.next()
                far = e_ >= 5
                s.op("pe", lambda e: e.matmul(pS[:, :], lhsT=kcT[:, ct * 128:(ct + 1) * 128], rhs=qb[:, hg, :], start=True, stop=far), reads=[kcd, qbd], writes=[pSd])
                if not far:
                    s.op("pe", lambda e: e.matmul(pS[:, :], lhsT=ident[:, :], rhs=bcmp[:, hg, e_, :], start=False, stop=True), reads=[cd], writes=[pSd])
                pT, pTd = pT_r.next()
                if far:
                    s.op("act", lambda e: e.activation(out=pT[:, :], in_=pS[:, :], func=AF.Exp, bias=ch[:, hg:hg + 1], scale=1.0), reads=[pSd, cd], writes=[pTd])
                else:
                    s.op("act", lambda e: e.activation(out=pT[:, :], in_=pS[:, :], func=AF.Exp), reads=[pSd], writes=[pTd])
                pts.append((pT, pTd))
            for sub in range(4):
                for ct in range(nct):
                    pT, pTd = pts[ct]
                    s.op("pe", lambda e: e.matmul(psA[sub][:, 0:193], lhsT=pT[:, sub * 128:(sub + 1) * 128], rhs=vce[:, ct, :], start=(ct == 0), stop=(ct == nct - 1)),
                         reads=[pTd, kcd], writes=[psA_d[sub]])
                finish_branch(psA[sub], psA_d[sub], hg, sub, 0, qc * 4 + sub, oacc, oaccd, True, imp, impd, hg == 0)
        nmT, nmTd = nmT_r.next()
        for sub in range(4):
            sc, scd = sc_r.next()
            s.op("dve", lambda e: e.tensor_tensor(out=sc[:, :], in0=imp[:, sub, :], in1=am[:, sub, :], op=ALU.add), reads=[impd, amd], writes=[scd])
            m8a, m8ad = m8_r.next()
            s.op("dve", lambda e: e.max(out=m8a[:, :], in_=sc[:, :]), reads=[scd], writes=[m8ad])
            sc2, sc2d = sc_r.next()
            s.op("dve", lambda e: e.match_replace(out=sc2[:, :], in_to_replace=m8a[:, :], in_values=sc[:, :], imm_value=-1e30), reads=[scd, m8ad], writes=[sc2d])
            m8b, m8bd = m8_r.next()
            s.op("dve", lambda e: e.max(out=m8b[:, :], in_=sc2[:, :]), reads=[sc2d], writes=[m8bd])
            s.op("dve", lambda e: e.tensor_scalar(out=sc2[:, :], in0=sc[:, :], scalar1=m8b[:, 7:8], scalar2=None, op0=ALU.is_ge), reads=[scd, m8bd, sc2d], writes=[sc2d])
            nm, nmd = nm_r.next()
            s.op("dve", lambda e: e.tensor_scalar(out=nm[:, :], in0=sc2[:, :], scalar1=-1.0, scalar2=-NEG, op0=ALU.add, op1=ALU.mult), reads=[sc2d], writes=[nmd])
            pt, ptd = psT.next()
            s.op("pe", lambda e: e.transpose(pt[:, 0:128], nm[:, :], ident[:, :]), reads=[nmd, cd], writes=[ptd])
            s.op("act", lambda e: e.activation(out=nmT[:, sub * 128:(sub + 1) * 128], in_=pt[:, 0:128], func=AF.Copy), reads=[ptd], writes=[nmTd])
        for br, (kk, vv) in ((1, (ks, vs)), (2, (kw, vw))):
            for hg in range(4):
                if br == 1:
                    kts = list(range(0, 4 * qc + 4))
                else:
                    kts = list(range(max(0, 4 * qc - 4), 4 * qc + 4))
                for i, kt in enumerate(kts):
                    d = kt - 4 * qc
                    pS, pSd = psS.next()
                    s.op("pe", lambda e: e.matmul(pS[:, :], lhsT=kk[:, kt * 128:(kt + 1) * 128], rhs=qb[:, hg, :], start=True, stop=False), reads=[kd, qbd], writes=[pSd])
                    use_ch = False
                    if br == 1:
                        if d >= -1:
                            s.op("pe", lambda e: e.matmul(pS[:, :], lhsT=ident[:, :], rhs=bsel[:, hg, d + 1, :], start=False, stop=False), reads=[cd], writes=[pSd])
                        else:
                            use_ch = True
                        s.op("pe", lambda e: e.matmul(pS[:, :], lhsT=E[:, kt, :], rhs=nmT[:, :], start=False, stop=True), reads=[cd, nmTd], writes=[pSd])
                    else:
                        if d >= 0:
                            s.op("pe", lambda e: e.matmul(pS[:, :], lhsT=ident[:, :], rhs=bsel[:, hg, d + 1, :], start=False, stop=True), reads=[cd], writes=[pSd])
                        elif d == -1:
                            s.op("pe", lambda e: e.matmul(pS[:, :], lhsT=ident[:, :], rhs=bw1[:, hg, :], start=False, stop=True), reads=[cd], writes=[pSd])
                        else:
                            use_ch = True
                            s.op("pe", lambda e: e.matmul(pS[:, :], lhsT=ident[:, :], rhs=mw[:, d + 4, :], start=False, stop=True), reads=[cd], writes=[pSd])
                    pT, pTd = pT_r.next()
                    if use_ch:
                        s.op("act", lambda e: e.activation(out=pT[:, :], in_=pS[:, :], func=AF.Exp, bias=ch[:, hg:hg + 1], scale=1.0), reads=[pSd, cd], writes=[pTd])
                    else:
                        s.op("act", lambda e: e.activation(out=pT[:, :], in_=pS[:, :], func=AF.Exp), reads=[pSd], writes=[pTd])
                    for sub in range(4):
                        s.op("pe", lambda e: e.matmul(psA[sub][:, 0:65], lhsT=pT[:, sub * 128:(sub + 1) * 128], rhs=vv[:, kt, :], start=(i == 0), stop=(i == len(kts) - 1)),
                             reads=[pTd, vd], writes=[psA_d[sub]])
                for sub in range(4):
                    finish_branch(psA[sub], psA_d[sub], hg, sub, br, qc * 4 + sub, oacc, oaccd, False)
        s.dma(o_out[:, qc * 4:(qc + 1) * 4, :], oacc[:, :, :], reads=[oaccd], is_output=True)
    return P


def nsa_consts(rel_bias, g):
    k = np.arange(128)[:, None]
    q = np.arange(512)[None, :]
    bsel = np.empty((128, 4, 5, 512), np.float32)
    bw1 = np.empty((128, 4, 512), np.float32)
    bcmp = np.empty((128, 4, 5, 512), np.float32)
    ch = np.empty((128, 4), np.float32)
    for hg in range(4):
        col = np.concatenate([rel_bias[:, 4 * g + hg], np.array([NEG], np.float32)]).astype(np.float32)
        ch[:, hg] = rel_bias[31, 4 * g + hg]
        for d in range(-1, 4):
            dist = q - k - 128 * d
            idx = np.where(dist >= 0, rel_bucket_np(dist), 32)
            bsel[:, hg, d + 1, :] = col[idx]
            if d == -1:
                idx2 = np.where((dist >= 0) & (dist < 512), rel_bucket_np(dist), 32)
                bw1[:, hg, :] = col[idx2]
        for e_ in range(5):
            dist = 512 * e_ + q - 16 * k - 31
            idx = np.where(dist >= 0, rel_bucket_np(dist), 32)
            bcmp[:, hg, e_, :] = col[idx]
    mw = np.empty((128, 3, 512), np.float32)
    for i, d in enumerate((-4, -3, -2)):
        dist = q - k - 128 * d
        mw[:, i, :] = np.where((dist >= 0) & (dist < 512), 0.0, NEG)
    return bsel, bw1, mw, bcmp, ch


def nsa_static():
    j = np.arange(128)[:, None, None]
    kt = np.arange(NKT)[None, :, None]
    kk = np.arange(128)[None, None, :]
    E = (j == 2 * kt + kk // 64).astype(np.float32)
    n = np.arange(512)
    start = n[:, None] * 16
    ss = np.arange(128)[None, :] * 64
    sm = ((start < ss + 64) & (start + 32 > ss)).astype(np.float32)
    sm[511, :] = 0.0
    selmap = np.ascontiguousarray(sm.reshape(4, 128, 128).transpose(1, 0, 2))
    t = np.arange(S)[:, None]
    jj = np.arange(128)[None, :]
    cur = t // 64
    valid = jj * 64 <= t
    forced = (jj == 0) | (jj == cur) | (jj == cur - 1)
    am = np.where(valid, np.where(forced, 1e4, 0.0), -1e9).astype(np.float32)
    addmask = np.ascontiguousarray(am.reshape(NKT, 128, 128).transpose(1, 0, 2))
    return E, selmap, np.eye(128, dtype=np.float32), addmask


def run_nsa(h, g_mix, w_in, cmp_pos, cmp_w1, cmp_b1, cmp_w2, cmp_b2, rel_bias):
    projT = assemble_T(run_proj(h, g_mix, w_in))
    E, selmap, ident, addmask = nsa_static()
    maps = []
    for c in range(NCORE):
        b, g = divmod(c, 4)
        pt = projT[b]
        q = pt[g * 256:(g + 1) * 256].reshape(4, 64, S).transpose(1, 0, 2)
        kv = pt[1024:1024 + 1536].reshape(6, 4, 64, S)[:, g]
        kvT = np.stack([kv[0], kv[1], kv[2], kv[4]], axis=1)
        vtok = np.stack([kv[3], kv[5]], axis=0)
        vtok = vtok.transpose(2, 0, 1).reshape(NKT, 128, 2, 64).transpose(1, 2, 0, 3)
        gl = pt[2560 + g * 12:2560 + (g + 1) * 12]
        gates = gl.T.reshape(NKT, 128, 12).transpose(1, 0, 2)
        bsel, bw1, mw, bcmp, ch = nsa_consts(rel_bias, g)
        maps.append({"q": np.ascontiguousarray(q), "kvT": np.ascontiguousarray(kvT), "vtok": np.ascontiguousarray(vtok),
                     "gates": np.ascontiguousarray(gates),
                     "cpos": np.ascontiguousarray(cmp_pos.transpose(2, 0, 1)),
                     "cw1": np.ascontiguousarray(cmp_w1.reshape(2, 32, 64, 64).transpose(2, 0, 1, 3)),
                     "cw2": np.ascontiguousarray(cmp_w2.transpose(1, 0, 2)),
                     "cb": np.ascontiguousarray(np.stack([cmp_b1, cmp_b2], axis=-1).transpose(1, 0, 2)),
                     "bias_sel": bsel, "bias_w1": bw1, "mask_w": mw, "bias_cmp": bcmp, "ch": ch,
                     "E": E, "selmap": selmap, "ident": ident, "addmask": addmask})
    res = run(("nsa_b",), build_nsa_b, maps)
    y = np.empty((NB, S, 1024), np.float32)
    for c in range(NCORE):
        b, g = divmod(c, 4)
        y[b, :, g * 256:(g + 1) * 256] = res[c]["o"].transpose(1, 0, 2).reshape(S, 256)
    return tok_shard_T(y)


def kernel(x, rel_bias, final_norm, mix_norm, ffn_norm, ffn_w_gate, ffn_w_up, ffn_w_down,
           lru_w_in, lru_conv_w, lru_conv_b, lru_w_a, lru_b_a, lru_w_x, lru_b_x, lru_lam, lru_w_out,
           nsa_w_in, nsa_cmp_pos, nsa_cmp_w1, nsa_cmp_b1, nsa_cmp_w2, nsa_cmp_b2, nsa_w_out,
           gla_w_in, gla_w_g2, gla_b_g2, gla_norm, gla_w_out):
    f = lambda a: np.asarray(a, dtype=np.float32)
    hl = tok_shard_T(f(x))
    fin = None
    for layer in range(4):
        kind, j = layer % 3, layer // 3
        if kind == 0:
            yl = run_lru(hl, f(mix_norm[layer]), f(lru_w_in[j]), f(lru_conv_w[j]), f(lru_conv_b[j]), f(lru_w_a[j]), f(lru_b_a[j]),
                         f(lru_w_x[j]), f(lru_b_x[j]), f(lru_lam[j]))
            w_out = f(lru_w_out[j])
        elif kind == 1:
            yl = run_nsa(hl, f(mix_norm[layer]), f(nsa_w_in[j]), f(nsa_cmp_pos[j]), f(nsa_cmp_w1[j]), f(nsa_cmp_b1[j]), f(nsa_cmp_w2[j]),
                         f(nsa_cmp_b2[j]), f(rel_bias))
            w_out = f(nsa_w_out[j])
        else:
            yl = run_gla(hl, f(mix_norm[layer]), f(gla_w_in[j]), f(gla_w_g2[j]), f(gla_b_g2[j]), f(gla_norm[j]))
            w_out = f(gla_w_out[j])
        hl, fin = run_tail(hl, yl, w_out, f(ffn_norm[layer]), f(ffn_w_gate[layer]), f(ffn_w_up[layer]), f(ffn_w_down[layer]),
                           f(final_norm) if layer == 3 else None)
    return tok_unshard_T(fin)
```
